# Optimizing a Trainium2 kernel written in Bass

```python
import math
import jax, jax.numpy as jnp
from jax import lax
import numpy as np

D_MODEL = 2048
BATCH = 2
SEQ = 4096
DEPTH = 2

GRID_W = 64
CTX_LEN = 256

BRANCH_WIDTH = 1024
N_BRANCH = 3

HY_WIDTH = BRANCH_WIDTH
HY_ORDER = 2
HY_CONV = 3
HY_EMB = 33
HY_BANDS = (HY_EMB - 1) // 2
HY_FILTER_HIDDEN = 64
HY_DECAY_TARGET = 1e-2
HY_FAST_DECAY = 0.3
HY_SLOW_DECAY = 1.5

HG_HEADS = 8
HG_DK = 128
HG_DV = BRANCH_WIDTH // HG_HEADS
HG_FWIDTH = HG_HEADS * HG_DK
HG_WIDTH = HG_HEADS * HG_DV
HG_F_MIN = 1e-30

DN_QK_HEADS = 4
DN_V_HEADS = 8
DN_DK = 128
DN_DV = BRANCH_WIDTH // DN_V_HEADS
DN_QK_WIDTH = DN_QK_HEADS * DN_DK
DN_WIDTH = DN_V_HEADS * DN_DV
DN_CONV = 3

CHUNK = 64
D_FF = 4 * D_MODEL
ALPHA = (2 * DEPTH) ** 0.25
BETA_INIT = (8 * DEPTH) ** -0.25
LN_EPS = 1e-5
RMS_EPS = 1e-6

STATE_SIZES = (HG_FWIDTH, HG_FWIDTH, HG_WIDTH, DN_QK_WIDTH, DN_WIDTH, 2 * DN_V_HEADS, 2 * DN_V_HEADS)
READ_SIZES = (HG_FWIDTH, HG_WIDTH, DN_QK_WIDTH, DN_WIDTH)
IN_SIZES = STATE_SIZES + READ_SIZES + ((HY_ORDER + 1) * HY_WIDTH, N_BRANCH * D_MODEL)
STATE_COLS = sum(STATE_SIZES)
IN_COLS = sum(IN_SIZES)

kernel_name = "hybrid_hyena_hgrn2_gdn_dit_block"


def _split(h, sizes):
    idx = [int(i) for i in np.cumsum(sizes)[:-1]]
    return jnp.split(h, idx, axis=-1)


def _flip(t):
    return None if t is None else jnp.flip(t, axis=1)


def _layernorm(x, g, b):
    xf = x.astype(jnp.float32)
    mu = jnp.mean(xf, -1, keepdims=True)
    var = jnp.mean(jnp.square(xf - mu), -1, keepdims=True)
    return ((xf - mu) * lax.rsqrt(var + LN_EPS) * g + b).astype(x.dtype)


def _rmsnorm(o, w):
    of = o.astype(jnp.float32)
    return of * lax.rsqrt(jnp.mean(of * of, -1, keepdims=True) + RMS_EPS) * w


def _l2norm(t):
    t = t.astype(jnp.float32)
    return t * lax.rsqrt(jnp.sum(t * t, -1, keepdims=True) + 1e-6)


def _modulate(x, shift, scale):
    return x * (1.0 + scale) + shift


def _short_conv(x, w, is_grid):
    b, l, ch = x.shape
    xs = x.reshape(b, l // GRID_W, GRID_W, ch) if is_grid else x
    k = w.shape[0]
    n = xs.shape[-2]
    xp = jnp.pad(xs, [(0, 0)] * (xs.ndim - 2) + [(k // 2, k // 2), (0, 0)])
    y = xp[..., 0:n, :] * w[0]
    for j in range(1, k):
        y = y + xp[..., j:j + n, :] * w[j]
    return y.reshape(b, l, ch)


def _hyena_filters(L, w1, b1, w2, b2, w3):
    f32 = jnp.float32
    t = jnp.linspace(0.0, 1.0, L, dtype=f32)[:, None]
    ang = (2.0 * math.pi / L) * jnp.arange(L, dtype=f32)[:, None]
    bands = jnp.linspace(1e-4, HY_BANDS - 1, HY_BANDS, dtype=f32)[None, :]
    feats = jnp.concatenate([t, jnp.cos(bands * ang), -jnp.sin(bands * ang)], axis=-1)
    hid = jnp.sin(feats @ w1.astype(f32) + b1.astype(f32))
    hid = jnp.sin(hid @ w2.astype(f32) + b2.astype(f32))
    h = (hid @ w3.astype(f32)).reshape(L, HY_ORDER, 2, HY_WIDTH)
    deltas = jnp.abs(jnp.linspace(math.log(HY_DECAY_TARGET) / HY_SLOW_DECAY,
                                  math.log(HY_DECAY_TARGET) / HY_FAST_DECAY, HY_WIDTH, dtype=f32))
    window = jnp.exp(-t * deltas)
    return h * window[:, None, None, :]


def _fft_long_conv(u, h_fwd, h_bwd, skip):
    L = u.shape[1]
    kfull = jnp.concatenate([h_fwd, jnp.zeros_like(h_fwd[:1]), h_bwd[:0:-1]], axis=0)
    uf = jnp.fft.rfft(u.astype(jnp.float32), n=2 * L, axis=1)
    kf = jnp.fft.rfft(kfull.astype(jnp.float32), n=2 * L, axis=0)
    y = jnp.fft.irfft(uf * kf[None], n=2 * L, axis=1)[:, :L]
    return (y + u.astype(jnp.float32) * skip).astype(u.dtype)


def _hyena(hh, conv_w, conv_b, filt_params, skip, is_grid):
    L = hh.shape[1]
    filt = _hyena_filters(L, *filt_params)
    z = _short_conv(hh, conv_w, is_grid) + conv_b
    v, *gates = jnp.split(z, HY_ORDER + 1, axis=-1)
    y = v
    for n, gate in enumerate(gates):
        y = gate * _fft_long_conv(y, filt[:, n, 0], filt[:, n, 1], skip[n])
    return y


def _to_chunks(t):
    b, l = t.shape[:2]
    t = t.reshape(b, l // CHUNK, CHUNK, *t.shape[2:])
    return jnp.moveaxis(t, (1, 2), (0, 3))


def _from_chunks(t):
    t = jnp.moveaxis(t, (0, 3), (1, 2))
    return t.reshape(t.shape[0], t.shape[1] * t.shape[2], *t.shape[3:])


def _gla_scan(q, k, v, log_f, s0):
    with_output = q is not None
    tri = jnp.tril(jnp.ones((CHUNK, CHUNK), bool))[:, :, None]
    xs = tuple(_to_chunks(t.astype(jnp.float32)) for t in (k, v, log_f))
    if with_output:
        xs = xs + (_to_chunks(q.astype(jnp.float32)),)

    def step(s, inp):
        kc, vc, gc = inp[:3]
        bc = jnp.cumsum(gc, axis=-2)
        b_last = bc[..., -1:, :]
        s_new = jnp.exp(b_last[..., 0, :])[..., None] * s + jnp.einsum(
            'bhsk,bhsv->bhkv', kc * jnp.exp(b_last - bc), vc)
        if not with_output:
            return s_new, None
        qc = inp[3]
        diff = bc[..., :, None, :] - bc[..., None, :, :]
        w = jnp.where(tri, jnp.exp(jnp.where(tri, diff, 0.0)), 0.0)
        att = jnp.einsum('bhtk,bhsk,bhtsk->bhts', qc, kc, w)
        o = jnp.einsum('bhtk,bhkv->bhtv', qc * jnp.exp(bc), s) + jnp.einsum('bhts,bhsv->bhtv', att, vc)
        return s_new, o

    s_fin, ys = lax.scan(step, s0, xs)
    return (_from_chunks(ys) if with_output else None), s_fin


def _delta_scan(q, k, v, g, beta, s0):
    with_output = q is not None
    tri = jnp.tril(jnp.ones((CHUNK, CHUNK), bool))
    strict = jnp.tril(jnp.ones((CHUNK, CHUNK), bool), -1)
    eye = jnp.eye(CHUNK, dtype=jnp.float32)
    xs = tuple(_to_chunks(t.astype(jnp.float32)) for t in (k, v, g, beta))
    if with_output:
        xs = xs + (_to_chunks(q.astype(jnp.float32)),)

    def step(s, inp):
        kc, vc, gc, bc = inp[:4]
        dv = vc.shape[-1]
        gcum = jnp.cumsum(gc, axis=-1)
        diff = gcum[..., :, None] - gcum[..., None, :]
        decay = jnp.where(tri, jnp.exp(jnp.where(tri, diff, 0.0)), 0.0)
        kb = kc * bc[..., None]
        a = eye + jnp.where(strict, jnp.einsum('bhtk,bhsk->bhts', kb, kc) * decay, 0.0)
        rhs = jnp.concatenate([vc * bc[..., None], kb * jnp.exp(gcum)[..., None]], axis=-1)
        sol = lax.linalg.triangular_solve(a, rhs, left_side=True, lower=True, unit_diagonal=False)
        u, w = sol[..., :dv], sol[..., dv:]
        v_new = u - jnp.einsum('bhtk,bhkv->bhtv', w, s)
        g_last = gcum[..., -1:]
        s_new = jnp.exp(g_last)[..., None] * s + jnp.einsum(
            'bhsk,bhsv->bhkv', kc * jnp.exp(g_last - gcum)[..., None], v_new)
        if not with_output:
            return s_new, None
        qc = inp[4]
        att = jnp.einsum('bhtk,bhsk->bhts', qc, kc) * decay
        o = jnp.einsum('bhtk,bhkv->bhtv', qc * jnp.exp(gcum)[..., None], s) + jnp.einsum(
            'bhts,bhsv->bhtv', att, v_new)
        return s_new, o

    s_fin, ys = lax.scan(step, s0, xs)
    return (_from_chunks(ys) if with_output else None), s_fin


def _hgrn2(q, f_f, f_b, i, g, lb_f, lb_b, norm_w, states):
    bsz, L = i.shape[:2]
    heads = lambda t: t.reshape(bsz, L, HG_HEADS, -1)

    def forget(fz, lb):
        fz = fz.astype(jnp.float32)
        f = lb + (1.0 - lb) * jax.nn.sigmoid(fz)
        return heads(jnp.log(jnp.maximum(f, HG_F_MIN))), heads((1.0 - lb) * jax.nn.sigmoid(-fz))

    lf_f, k_f = forget(f_f, lb_f)
    lf_b, k_b = forget(f_b, lb_b)
    ih = heads(i)
    qh = None if q is None else heads(jax.nn.silu(q))
    o_f, s_f = _gla_scan(qh, k_f, ih, lf_f, states[0])
    o_b, s_b = _gla_scan(_flip(qh), _flip(k_b), _flip(ih), _flip(lf_b), states[1])
    if q is None:
        return None, (s_f, s_b)
    o = _rmsnorm(o_f + _flip(o_b), norm_w) * jax.nn.sigmoid(heads(g).astype(jnp.float32))
    return o.reshape(bsz, L, HG_WIDTH).astype(i.dtype), (s_f, s_b)


def _gated_deltanet(q_raw, k_raw, v_raw, z, a_all, b_all, conv_q, conv_k, conv_v,
                    a_log, dt_bias, norm_w, states, is_grid):
    bsz, L = k_raw.shape[:2]
    rep = DN_V_HEADS // DN_QK_HEADS
    k = _l2norm(jax.nn.silu(_short_conv(k_raw, conv_k, is_grid)).reshape(bsz, L, DN_QK_HEADS, DN_DK))
    k = jnp.repeat(k, rep, axis=2)
    v = jax.nn.silu(_short_conv(v_raw, conv_v, is_grid)).reshape(bsz, L, DN_V_HEADS, DN_DV)
    a_f, a_b = jnp.split(a_all.astype(jnp.float32), 2, axis=-1)
    b_f, b_b = jnp.split(b_all.astype(jnp.float32), 2, axis=-1)
    g_f = -jnp.exp(a_log[0]) * jax.nn.softplus(a_f + dt_bias[0])
    g_b = -jnp.exp(a_log[1]) * jax.nn.softplus(a_b + dt_bias[1])
    q = None
    if q_raw is not None:
        q = _l2norm(jax.nn.silu(_short_conv(q_raw, conv_q, is_grid)).reshape(bsz, L, DN_QK_HEADS, DN_DK))
        q = jnp.repeat(q * DN_DK ** -0.5, rep, axis=2)
    o_f, s_f = _delta_scan(q, k, v, g_f, jax.nn.sigmoid(b_f), states[0])
    o_b, s_b = _delta_scan(_flip(q), _flip(k), _flip(v), _flip(g_b), _flip(jax.nn.sigmoid(b_b)), states[1])
    if q is None:
        return None, (s_f, s_b)
    zh = z.reshape(bsz, L, DN_V_HEADS, DN_DV).astype(jnp.float32)
    o = _rmsnorm(o_f + _flip(o_b), norm_w) * jax.nn.silu(zh)
    return o.reshape(bsz, L, DN_WIDTH).astype(k_raw.dtype), (s_f, s_b)


def _token_mixers(u, lp, init_states, is_grid, with_output):
    bsz, L, _ = u.shape
    if with_output:
        parts = _split(u @ lp["w_in"], IN_SIZES)
    else:
        parts = _split(u @ lp["w_in"][:, :STATE_COLS], STATE_SIZES) + [None] * (len(IN_SIZES) - len(STATE_SIZES))
    hg_ff, hg_fb, hg_i, dn_k, dn_v, dn_a, dn_b, hg_q, hg_g, dn_q, dn_z, hy_in, gate_in = parts
    hg_out, hg_states = _hgrn2(hg_q, hg_ff, hg_fb, hg_i, hg_g, lp["lb_fwd"], lp["lb_bwd"],
                               lp["hg_norm_w"], init_states[0])
    dn_out, dn_states = _gated_deltanet(dn_q, dn_k, dn_v, dn_z, dn_a, dn_b, lp["dn_conv_q"], lp["dn_conv_k"],
                                        lp["dn_conv_v"], lp["dn_a_log"], lp["dn_dt_bias"], lp["dn_norm_w"],
                                        init_states[1], is_grid)
    states = (hg_states, dn_states)
    if not with_output:
        return None, states
    hy_out = _hyena(hy_in, lp["hy_conv_w"], lp["hy_conv_b"], lp["hy_filt"], lp["hy_skip"], is_grid)
    branches = jnp.stack([hy_out, hg_out, dn_out], axis=2)
    proj = jnp.einsum('blgc,gcd->blgd', branches, lp["w_branch"])
    gate = jax.nn.sigmoid(gate_in.reshape(bsz, L, N_BRANCH, D_MODEL))
    y = jnp.sum(gate * proj, axis=2) @ lp["w_out"]
    return y, states


def _sq_relu_mlp(u, w1, w2):
    return jnp.square(jax.nn.relu(u @ w1)) @ w2


def setup_inputs(seed: int = 0) -> dict:
    key = jax.random.key(seed)
    ks = jax.random.split(key, 40)
    f32 = jnp.float32
    nrm = lambda k, shape, s: s * jax.random.normal(k, shape, f32)
    dt = jnp.exp(jax.random.uniform(ks[22], (DEPTH, 2, DN_V_HEADS), f32, math.log(1e-3), math.log(1e-1)))
    return {
        "x": nrm(ks[0], (BATCH, SEQ, D_MODEL), 1.0),
        "c": nrm(ks[1], (BATCH, D_MODEL), 1.0),
        "ctx": nrm(ks[2], (BATCH, CTX_LEN, D_MODEL), 1.0),
        "c_ctx": nrm(ks[3], (D_MODEL,), 1.0),
        "w_ada": nrm(ks[4], (DEPTH, D_MODEL, 6 * D_MODEL), 0.5 * D_MODEL ** -0.5),
        "b_ada": nrm(ks[5], (DEPTH, 6 * D_MODEL), 0.02),
        "w_in": nrm(ks[6], (DEPTH, D_MODEL, IN_COLS), D_MODEL ** -0.5),
        "hy_conv_w": nrm(ks[7], (DEPTH, HY_CONV, (HY_ORDER + 1) * HY_WIDTH), HY_CONV ** -0.5),
        "hy_conv_b": nrm(ks[8], (DEPTH, (HY_ORDER + 1) * HY_WIDTH), 0.02),
        "hy_filt_w1": nrm(ks[9], (DEPTH, HY_EMB, HY_FILTER_HIDDEN), HY_EMB ** -0.5),
        "hy_filt_b1": nrm(ks[10], (DEPTH, HY_FILTER_HIDDEN), 0.02),
        "hy_filt_w2": nrm(ks[11], (DEPTH, HY_FILTER_HIDDEN, HY_FILTER_HIDDEN), HY_FILTER_HIDDEN ** -0.5),
        "hy_filt_b2": nrm(ks[12], (DEPTH, HY_FILTER_HIDDEN), 0.02),
        "hy_filt_w3": nrm(ks[13], (DEPTH, HY_FILTER_HIDDEN, HY_ORDER * 2 * HY_WIDTH), 0.1 * HY_FILTER_HIDDEN ** -0.5),
        "hy_skip": nrm(ks[14], (DEPTH, HY_ORDER, HY_WIDTH), 1.0),
        "hg_lb_logits": nrm(ks[15], (2, DEPTH, HG_FWIDTH), 0.1),
        "hg_norm_w": 1.0 + nrm(ks[16], (DEPTH, HG_DV), 0.02),
        "dn_conv_q": nrm(ks[17], (DEPTH, DN_CONV, DN_QK_WIDTH), DN_CONV ** -0.5),
        "dn_conv_k": nrm(ks[18], (DEPTH, DN_CONV, DN_QK_WIDTH), DN_CONV ** -0.5),
        "dn_conv_v": nrm(ks[19], (DEPTH, DN_CONV, DN_WIDTH), DN_CONV ** -0.5),
        "dn_a_log": jnp.log(jax.random.uniform(ks[20], (DEPTH, 2, DN_V_HEADS), f32, 1.0, 16.0)),
        "dn_dt_bias": dt + jnp.log(-jnp.expm1(-dt)),
        "dn_norm_w": 1.0 + nrm(ks[21], (DEPTH, DN_DV), 0.02),
        "w_branch": nrm(ks[23], (DEPTH, N_BRANCH, BRANCH_WIDTH, D_MODEL), BRANCH_WIDTH ** -0.5),
        "w_out": nrm(ks[24], (DEPTH, D_MODEL, D_MODEL), BETA_INIT * D_MODEL ** -0.5),
        "ln1_g": 1.0 + nrm(ks[25], (DEPTH, D_MODEL), 0.02),
        "ln1_b": nrm(ks[26], (DEPTH, D_MODEL), 0.02),
        "w_ff1": nrm(ks[27], (DEPTH, D_MODEL, D_FF), D_MODEL ** -0.5),
        "w_ff2": nrm(ks[28], (DEPTH, D_FF, D_MODEL), BETA_INIT * D_FF ** -0.5),
        "ln2_g": 1.0 + nrm(ks[29], (DEPTH, D_MODEL), 0.02),
        "ln2_b": nrm(ks[30], (DEPTH, D_MODEL), 0.02),
    }


def reference(x, c, ctx, c_ctx, w_ada, b_ada, w_in, hy_conv_w, hy_conv_b, hy_filt_w1, hy_filt_b1,
              hy_filt_w2, hy_filt_b2, hy_filt_w3, hy_skip, hg_lb_logits, hg_norm_w, dn_conv_q, dn_conv_k,
              dn_conv_v, dn_a_log, dn_dt_bias, dn_norm_w, w_branch, w_out, ln1_g, ln1_b, w_ff1, w_ff2,
              ln2_g, ln2_b):
    f32 = jnp.float32
    bsz = x.shape[0]
    p = jax.nn.softmax(hg_lb_logits.astype(f32), axis=1)
    lower_bounds = jnp.cumsum(p, axis=1) - p[:, :1]
    h_ctx = ctx
    for l in range(DEPTH):
        last = l == DEPTH - 1
        lp = {
            "w_in": w_in[l], "hy_conv_w": hy_conv_w[l], "hy_conv_b": hy_conv_b[l],
            "hy_filt": (hy_filt_w1[l], hy_filt_b1[l], hy_filt_w2[l], hy_filt_b2[l], hy_filt_w3[l]),
            "hy_skip": hy_skip[l], "lb_fwd": lower_bounds[0, l], "lb_bwd": lower_bounds[1, l],
            "hg_norm_w": hg_norm_w[l], "dn_conv_q": dn_conv_q[l], "dn_conv_k": dn_conv_k[l],
            "dn_conv_v": dn_conv_v[l], "dn_a_log": dn_a_log[l], "dn_dt_bias": dn_dt_bias[l],
            "dn_norm_w": dn_norm_w[l], "w_branch": w_branch[l], "w_out": w_out[l],
        }
        zero_hg = jnp.zeros((bsz, HG_HEADS, HG_DK, HG_DV), f32)
        zero_dn = jnp.zeros((bsz, DN_V_HEADS, DN_DK, DN_DV), f32)
        init = ((zero_hg, zero_hg), (zero_dn, zero_dn))
        n_mod = 2 if last else 6
        mod_c = jax.nn.silu(c_ctx) @ w_ada[l][:, :n_mod * D_MODEL] + b_ada[l][:n_mod * D_MODEL]
        mods_c = jnp.split(mod_c, n_mod)
        y_ctx, ctx_states = _token_mixers(_modulate(h_ctx, mods_c[0], mods_c[1]), lp, init, False, not last)
        if not last:
            h_ctx = _layernorm(ALPHA * h_ctx + mods_c[2] * y_ctx, ln1_g[l], ln1_b[l])
            h_ctx = _layernorm(ALPHA * h_ctx + mods_c[5] * _sq_relu_mlp(_modulate(h_ctx, mods_c[3], mods_c[4]),
                                                                          w_ff1[l], w_ff2[l]), ln2_g[l], ln2_b[l])
        mod = jax.nn.silu(c) @ w_ada[l] + b_ada[l]
        sh1, sc1, g1, sh2, sc2, g2 = [m[:, None, :] for m in jnp.split(mod, 6, axis=-1)]
        y, _ = _token_mixers(_modulate(x, sh1, sc1), lp, ctx_states, True, True)
        x = _layernorm(ALPHA * x + g1 * y, ln1_g[l], ln1_b[l])
        x = _layernorm(ALPHA * x + g2 * _sq_relu_mlp(_modulate(x, sh2, sc2), w_ff1[l], w_ff2[l]),
                       ln2_g[l], ln2_b[l])
    return x
```

```python
import os
import numpy as np
from contextlib import ExitStack
import concourse.bass as bass
import concourse.mybir as mybir
from concourse.bass_utils import run_bass_kernel_spmd

F32 = mybir.dt.float32
BF16 = mybir.dt.bfloat16
AF = mybir.ActivationFunctionType
ALU = mybir.AluOpType
AX = mybir.AxisListType

ENGS = ("pe", "act", "dve", "pool", "sp")


class Buf:
    def __init__(self, t, name):
        self.t = t
        self.name = name
        self.last_w = None
        self.readers = []
        self.dma_cnt = 0

    def __getitem__(self, idx):
        return View(self, self.t[idx])

    def ap(self):
        return View(self, self.t[:])


class View:
    def __init__(self, buf, ap):
        self.buf = buf
        self.ap = ap

    def __getitem__(self, idx):
        return View(self.buf, self.ap[idx])

    def rearrange(self, *a, **k):
        return View(self.buf, self.ap.rearrange(*a, **k))

    def bitcast(self, dt):
        return View(self.buf, self.ap.bitcast(dt))


def _ap(x):
    return x.ap if isinstance(x, View) else x


class Prog:
    def __init__(self, nc):
        self.nc = nc
        self.es = ExitStack()
        self.q = {e: [] for e in ENGS}
        self.cnt = {e: 0 for e in ENGS}
        self.known = {e: {} for e in ENGS}
        self.sems = {}
        self.nbuf = 0
        for e in ENGS:
            self.sems[e] = self.es.enter_context(nc.semaphore("sem_" + e))

    def sbuf(self, shape, dt, name=None):
        self.nbuf += 1
        name = name or f"sb{self.nbuf}"
        t = self.es.enter_context(self.nc.sbuf_tensor(name, list(shape), dt))
        return Buf(t, name)

    def psum(self, shape, dt=F32, name=None):
        self.nbuf += 1
        name = name or f"ps{self.nbuf}"
        t = self.es.enter_context(self.nc.psum_tensor(name, list(shape), dt))
        return Buf(t, name)

    def dram(self, name, shape, dt, kind="Internal"):
        t = self.nc.dram_tensor(name, list(shape), dt, kind=kind)
        return Buf(t.ap(), name)

    def alias(self, buf, t, name):
        return Buf(t, name)

    def _waits(self, eng, reads, writes, skip_same_pe=True):
        need = {}

        def add(tok):
            if tok is None:
                return
            k, v = tok
            if eng == "pe" and k == "pe":
                return
            if need.get(k, 0) < v:
                need[k] = v

        for b in reads:
            add(b.last_w)
        for b in writes:
            add(b.last_w)
            for r in b.readers:
                add(r)
        out = []
        kn = self.known[eng]
        for k, v in need.items():
            if kn.get(k, 0) >= v:
                continue
            kn[k] = v
            out.append((k, v))
        return out

    def _bufs(self, xs):
        out = []
        for x in xs:
            if x is None:
                continue
            b = x.buf if isinstance(x, View) else x
            if isinstance(b, Buf) and b not in out:
                out.append(b)
        return out

    def op(self, eng, fn, reads=(), writes=()):
        reads = self._bufs(reads)
        writes = self._bufs(writes)
        waits = self._waits(eng, reads, writes)
        self.cnt[eng] += 1
        tok = (eng, self.cnt[eng])
        self._emit(eng, waits, fn, (eng, 1))
        for b in reads:
            if b not in writes:
                b.readers.append(tok)
        for b in writes:
            b.last_w = tok
            b.readers = []
        return tok

    def dma(self, eng, out, in_, **kw):
        ob = self._bufs([out])[0]
        ib = self._bufs([in_])[0]
        waits = self._waits(eng, [ib], [ob])
        key = "dma_" + ob.name
        if key not in self.sems:
            self.sems[key] = self.es.enter_context(self.nc.semaphore("s_" + key))
        ob.dma_cnt += 1
        tok = (key, 16 * ob.dma_cnt)
        o_ap, i_ap = _ap(out), _ap(in_)
        self._emit(eng, waits, lambda e: e.dma_start(out=o_ap, in_=i_ap, **kw), (key, 16))
        ib.readers.append(tok)
        ob.last_w = tok
        ob.readers = []
        return tok

    def dma_group(self, items):
        ob = self._bufs([items[0][1]])[0]
        key = "dma_" + ob.name
        if key not in self.sems:
            self.sems[key] = self.es.enter_context(self.nc.semaphore("s_" + key))
        prev_w, prev_r = ob.last_w, list(ob.readers)
        toks = []
        for eng, out, in_ in items:
            ib = self._bufs([in_])[0]
            ob.last_w, ob.readers = prev_w, prev_r
            waits = self._waits(eng, [ib], [ob])
            ob.dma_cnt += 1
            o_ap, i_ap = _ap(out), _ap(in_)
            self._emit(eng, waits, (lambda e, o=o_ap, i=i_ap: e.dma_start(out=o, in_=i)), (key, 16))
            toks.append((ib, None))
        tok = (key, 16 * ob.dma_cnt)
        for ib, _ in toks:
            ib.readers.append(tok)
        ob.last_w = tok
        ob.readers = []
        return tok

    def wait_all(self, eng, bufs):
        waits = self._waits(eng, self._bufs(bufs), [])
        self._emit(eng, waits, None, None)

    def _emit(self, e, waits, fn, inc):
        eng = getattr(self.nc, {"pe": "tensor", "act": "scalar", "dve": "vector", "pool": "gpsimd", "sp": "sync"}[e])
        for k, v in waits:
            eng.wait_ge(self.sems[k], v)
        if fn is not None:
            ins = fn(eng)
            ins.then_inc(self.sems[inc[0]], inc[1])
        self.ninst = getattr(self, "ninst", 0) + 1

    def emit(self):
        self.es.close()


    def mm(self, out, lhsT, rhs, start=True, stop=True):
        return self.op("pe", lambda e: e.matmul(_ap(out), _ap(lhsT), _ap(rhs), start=start, stop=stop),
                       reads=[lhsT, rhs], writes=[out])

    def transpose(self, out, in_, ident):
        return self.op("pe", lambda e: e.transpose(_ap(out), _ap(in_), _ap(ident)), reads=[in_, ident], writes=[out])

    def act(self, out, in_, func, scale=1.0, bias=0.0, eng="act", accum_out=None):
        kw = {}
        if accum_out is not None:
            kw["accum_out"] = _ap(accum_out)
        return self.op(eng, lambda e: e.activation(out=_ap(out), in_=_ap(in_), func=func, scale=_ap(scale), bias=_ap(bias), **kw),
                       reads=[in_, scale, bias], writes=[out, accum_out])

    def tt(self, out, in0, in1, op, eng="dve"):
        return self.op(eng, lambda e: e.tensor_tensor(out=_ap(out), in0=_ap(in0), in1=_ap(in1), op=op),
                       reads=[in0, in1], writes=[out])

    def ts(self, out, in0, s1, op0, s2=None, op1=None, eng="dve"):
        kw = dict(scalar2=None)
        if op1 is not None:
            kw = dict(scalar2=_ap(s2), op1=op1)
        return self.op(eng, lambda e: e.tensor_scalar(out=_ap(out), in0=_ap(in0), scalar1=_ap(s1), op0=op0, **kw),
                       reads=[in0, s1, s2], writes=[out])

    def stt(self, out, in0, scalar, in1, op0, op1, eng="dve"):
        return self.op(eng, lambda e: e.scalar_tensor_tensor(out=_ap(out), in0=_ap(in0), scalar=_ap(scalar), in1=_ap(in1), op0=op0, op1=op1),
                       reads=[in0, scalar, in1], writes=[out])

    def copy(self, out, in_, eng="dve"):
        return self.op(eng, lambda e: e.tensor_copy(out=_ap(out), in_=_ap(in_)), reads=[in_], writes=[out])

    def memset(self, out, val, eng="dve"):
        return self.op(eng, lambda e: e.memset(_ap(out), val), reads=[], writes=[out])

    def recip(self, out, in_):
        return self.op("dve", lambda e: e.reciprocal(out=_ap(out), in_=_ap(in_)), reads=[in_], writes=[out])

    def scan(self, out, d0, d1, initial, op0, op1):
        return self.op("dve", lambda e: e.tensor_tensor_scan(out=_ap(out), data0=_ap(d0), data1=_ap(d1), initial=_ap(initial), op0=op0, op1=op1),
                       reads=[d0, d1, initial], writes=[out])


D = 2048
ALPHA_C = (2 * 2) ** 0.25
LN_EPS = 1e-5
RMS_EPS = 1e-6


class Ring:
    def __init__(self, bufs):
        self.bufs = bufs
        self.i = 0

    def next(self):
        b = self.bufs[self.i % len(self.bufs)]
        self.i += 1
        return b


def new_prog():
    nc = bass.Bass("TRN2", target_bir_lowering=False)
    return nc, Prog(nc)


def load_w_cast(p, dst, src, nK, eng="pool"):
    items = []
    for k in range(nK):
        items.append((eng, dst[:, k, :], src[k * 128:(k + 1) * 128, :]))
    p.dma_group(items)


def build_mod():
    nc, p = new_prog()
    vT = p.dram("vT", [128, 48], F32, "ExternalInput")
    w = p.dram("w", [2, D, 1536], F32, "ExternalInput")
    b3 = p.dram("b3", [2, 3, 1536], F32, "ExternalInput")
    out = p.dram("mod", [2, 3, 1536], F32, "ExternalOutput")
    v_sb = p.sbuf([128, 48], F32)
    sv = p.sbuf([128, 48], F32)
    p.dma("sp", v_sb[:, :], vT[:, :])
    p.act(sv[:, :], v_sb[:, :], AF.Silu)
    wr = Ring([p.sbuf([128, 16, 512], F32) for _ in range(2)])
    br = Ring([p.sbuf([3, 512], F32) for _ in range(2)])
    orr = Ring([p.sbuf([3, 512], F32) for _ in range(2)])
    pr = Ring([p.psum([128, 512]) for _ in range(2)])
    for l in range(2):
        for ct in range(3):
            wt = wr.next()
            p.dma_group([("sp", wt[:, k, :], w[l, k * 128:(k + 1) * 128, ct * 512:(ct + 1) * 512]) for k in range(16)])
            bt = br.next()
            p.dma("sp", bt[:, :], b3[l, :, ct * 512:(ct + 1) * 512])
            ps = pr.next()
            for k in range(16):
                p.mm(ps[0:3, :], sv[:, k * 3:k * 3 + 3], wt[:, k, :], start=(k == 0), stop=(k == 15))
            o = orr.next()
            p.tt(o[:, :], ps[0:3, :], bt[:, :], ALU.add)
            p.dma("sp", out[l, :, ct * 512:(ct + 1) * 512], o[:, :])
    p.wait_all("sp", [out])
    p.emit()
    return nc


def build_inproj(NC, NTOK=8704):
    nc, p = new_prog()
    NT = NTOK // 512
    NCT = NC // 128
    xT = p.dram("xT", [D, NTOK], F32, "ExternalInput")
    w = p.dram("w", [D, NC], F32, "ExternalInput")
    scd = p.dram("sc", [128, 48], F32, "ExternalInput")
    shd = p.dram("sh", [128, 48], F32, "ExternalInput")
    out = p.dram("PT", [NC, NTOK], F32, "ExternalOutput")
    cwd = p.dram("cw", [128, 39], F32, "ExternalInput")
    cbd = p.dram("cb", [128, 13], F32, "ExternalInput")
    cw, cb = p.sbuf([128, 39], F32, "cw_sb"), p.sbuf([128, 13], F32, "cb_sb")
    p.dma("sp", cw[:, :], cwd[:, :])
    p.dma("sp", cb[:, :], cbd[:, :])
    CONV = (3, 4, 7, 9, 10, 11) if NC == 13 * 128 else ()
    SILU = (3, 4, 7)
    w_sb = p.sbuf([128, 16, NC], BF16)
    load_w_cast(p, w_sb, w, 16)
    sc = p.sbuf([128, 48], F32)
    sh = p.sbuf([128, 48], F32)
    p.dma("sp", sc[:, :], scd[:, :])
    p.dma("sp", sh[:, :], shd[:, :])
    p.ts(sc[:, :], sc[:, :], 1.0, ALU.add)
    xr = Ring([p.sbuf([128, 16, 512], F32) for _ in range(2)])
    ur = Ring([p.sbuf([128, 16, 512], BF16) for _ in range(2)])
    orr = Ring([p.sbuf([128, 512], F32) for _ in range(4)])
    pr = Ring([p.psum([128, 512]) for _ in range(6)])
    n = 0
    for t in range(NT):
        g = 0 if t == 0 else (1 if t <= 8 else 2)
        x = xr.next()
        p.dma_group([("sp", x[:, k, :], xT[k * 128:(k + 1) * 128, t * 512:(t + 1) * 512]) for k in range(16)])
        u = ur.next()
        for k in range(16):
            j = g * 16 + k
            if k % 2 == 0:
                p.act(u[:, k, :], x[:, k, :], AF.Identity, scale=sc[:, j:j + 1], bias=sh[:, j:j + 1])
            else:
                p.ts(u[:, k, :], x[:, k, :], sc[:, j:j + 1], ALU.mult, sh[:, j:j + 1], ALU.add)
        for c in range(NCT):
            ps = pr.next()
            for k in range(16):
                p.mm(ps[:, :], w_sb[:, k, c * 128:(c + 1) * 128], u[:, k, :], start=(k == 0), stop=(k == 15))
            o = orr.next()
            if c in CONV:
                rw = 256 if t == 0 else 64
                o3 = o[:, :].rearrange("p (r w) -> p r w", w=rw)
                p3 = ps[:, :].rearrange("p (r w) -> p r w", w=rw)
                p.act(o[:, :], ps[:, :], AF.Identity, scale=cw[:, c * 3 + 1:c * 3 + 2], bias=cb[:, c:c + 1])
                p.stt(o3[:, :, 1:rw], p3[:, :, 0:rw - 1], cw[:, c * 3:c * 3 + 1], o3[:, :, 1:rw], ALU.mult, ALU.add)
                p.stt(o3[:, :, 0:rw - 1], p3[:, :, 1:rw], cw[:, c * 3 + 2:c * 3 + 3], o3[:, :, 0:rw - 1], ALU.mult, ALU.add)
                if c in SILU:
                    p.act(o[:, :], o[:, :], AF.Silu)
            elif n % 2 == 0:
                p.copy(o[:, :], ps[:, :], eng="dve")
            else:
                p.act(o[:, :], ps[:, :], AF.Copy)
            n += 1
            p.dma("sp", out[c * 128:(c + 1) * 128, t * 512:(t + 1) * 512], o[:, :])
    p.wait_all("sp", [out])
    p.emit()
    return nc


OFF = {}
_o = 0
for _n, _s in [("hg_ff", 1024), ("hg_fb", 1024), ("hg_i", 1024), ("dn_k", 512), ("dn_v", 1024), ("dn_a", 16), ("dn_b", 16),
               ("hg_q", 1024), ("hg_g", 1024), ("dn_q", 512), ("dn_z", 1024), ("hy_in", 3072), ("gate", 6144)]:
    OFF[_n] = _o
    _o += _s
TILE_NAMES = ["hg_ff", "hg_fb", "hg_i", "dn_k", "dn_v", "hg_q", "hg_g", "dn_q", "dn_z", "hy_v", "hy_g1", "hy_g2", "ab"]


def head_cols(h):
    cols = []
    r = np.arange(128)
    cols.append(OFF["hg_ff"] + h * 128 + r)
    cols.append(OFF["hg_fb"] + h * 128 + r)
    cols.append(OFF["hg_i"] + h * 128 + r)
    cols.append(OFF["dn_k"] + (h // 2) * 128 + r)
    cols.append(OFF["dn_v"] + h * 128 + r)
    cols.append(OFF["hg_q"] + h * 128 + r)
    cols.append(OFF["hg_g"] + h * 128 + r)
    cols.append(OFF["dn_q"] + (h // 2) * 128 + r)
    cols.append(OFF["dn_z"] + h * 128 + r)
    for j in range(3):
        cols.append(OFF["hy_in"] + j * 1024 + h * 128 + r)
    return np.concatenate(cols)


def head_weight(w_in_l, h):
    wh = np.zeros((D, 13 * 128), np.float32)
    wh[:, :12 * 128] = w_in_l[:, head_cols(h)]
    wh[:, 12 * 128 + 0] = w_in_l[:, OFF["dn_a"] + h]
    wh[:, 12 * 128 + 1] = w_in_l[:, OFF["dn_a"] + 8 + h]
    wh[:, 12 * 128 + 2] = w_in_l[:, OFF["dn_b"] + h]
    wh[:, 12 * 128 + 3] = w_in_l[:, OFF["dn_b"] + 8 + h]
    return wh


def pk(vec):
    return np.ascontiguousarray(vec.reshape(16, 128).T)


def run(nc, maps):
    return run_bass_kernel_spmd(nc, maps, core_ids=list(range(8))).results


def build_post(with_ctx, dbg=False):
    nc, p = new_prog()
    tts = [(0, 64, 0), (64, 512, 1), (576, 512, 1)] if with_ctx else [(0, 512, 1), (512, 512, 1)]
    T = tts[-1][0] + tts[-1][1]
    di = lambda n, s: p.dram(n, s, F32, "ExternalInput")
    xT = di("xT", [D, T])
    modd = di("mods", [128, 192])
    hgf, hgb, hgg = di("hgf", [1024, T]), di("hgb", [1024, T]), di("hgg", [1024, T])
    dnf, dnb, dnz = di("dnf", [1024, T]), di("dnb", [1024, T]), di("dnz", [1024, T])
    hy = di("hy", [1024, T])
    nwd = di("nw", [128, 2])
    wg = di("wg", [D, 6144])
    wb = di("wb", [3072, D])
    wo = di("wo", [D, D])
    lnd = di("ln", [128, 64])
    wf1 = di("wf1", [D, 4 * D])
    wf2 = di("wf2", [4 * D, D])
    outT = p.dram("oT", [D, T], F32, "ExternalOutput")

    mods = p.sbuf([128, 192], F32)
    ln = p.sbuf([128, 64], F32)
    nw = p.sbuf([128, 2], F32)
    p.dma("sp", mods[:, :], modd[:, :])
    p.dma("sp", ln[:, :], lnd[:, :])
    p.dma("sp", nw[:, :], nwd[:, :])
    mods1 = p.sbuf([128, 192], F32)
    p.ts(mods1[:, :], mods[:, :], 1.0, ALU.add)
    ones = p.sbuf([128, 128], F32)
    p.memset(ones[:, :], 1.0)

    def mcol(grp, j, k, plus1=False):
        i = (grp * 6 + j) * 16 + k
        return (mods1 if plus1 else mods)[:, i:i + 1]

    x = p.sbuf([128, 16, T], F32, "x_res")
    u = p.sbuf([128, 16, T], BF16, "u_bf")
    m = p.sbuf([128, 16, T], BF16, "m_bf")
    xbf = x[:, :, :].rearrange("p k t -> p (k t)").bitcast(BF16)

    def brv(j, s, n):
        return xbf[:, j * T + s: j * T + s + n]

    pr = Ring([p.psum([128, 512], name=f"bank{i}") for i in range(8)])
    tmp_r = Ring([p.sbuf([128, 512], F32) for _ in range(10)])
    wr16 = Ring([p.sbuf([128, 16, 256], BF16) for _ in range(3)])
    wr8 = Ring([p.sbuf([128, 8, 256], BF16) for _ in range(3)])

    def modulate(j_sh, j_sc):
        for k in range(16):
            for (s, n, g) in tts:
                if k % 2 == 0:
                    p.act(u[:, k, s:s + n], x[:, k, s:s + n], AF.Identity, scale=mcol(g, j_sc, k, True), bias=mcol(g, j_sh, k))
                else:
                    p.ts(u[:, k, s:s + n], x[:, k, s:s + n], mcol(g, j_sc, k, True), ALU.mult, mcol(g, j_sh, k), ALU.add)

    def gemm(wt, c0, nK, rhs, evac):
        for ti, (s, n, g) in enumerate(tts):
            ps = pr.next()
            for k in range(nK):
                p.mm(ps[:, 0:n], wt[:, k, c0:c0 + 128], rhs(k, s, n), start=(k == 0), stop=(k == nK - 1))
            evac(ti, ps)

    def stats_ln(src, eps, nch, dst_fn):
        for (s, n, g) in tts:
            ps1, ps2 = pr.next(), pr.next()
            for k in range(nch):
                sq = tmp_r.next()
                p.act(sq[:, 0:n], src[:, k, s:s + n], AF.Square)
                p.mm(ps1[:, 0:n], ones[:, :], src[:, k, s:s + n], start=(k == 0), stop=(k == nch - 1))
                p.mm(ps2[:, 0:n], ones[:, :], sq[:, 0:n], start=(k == 0), stop=(k == nch - 1))
            mean, msq, var, rstd, nmr = [tmp_r.next() for _ in range(5)]
            inv = 1.0 / (128 * nch)
            p.act(mean[:, 0:n], ps1[:, 0:n], AF.Copy, scale=inv)
            p.act(msq[:, 0:n], mean[:, 0:n], AF.Square)
            p.stt(var[:, 0:n], ps2[:, 0:n], inv, msq[:, 0:n], ALU.mult, ALU.subtract)
            p.ts(var[:, 0:n], var[:, 0:n], eps, ALU.add)
            p.act(var[:, 0:n], var[:, 0:n], AF.Sqrt)
            p.recip(rstd[:, 0:n], var[:, 0:n])
            p.stt(nmr[:, 0:n], mean[:, 0:n], -1.0, rstd[:, 0:n], ALU.mult, ALU.mult)
            for k in range(nch):
                t = msq
                p.tt(t[:, 0:n], src[:, k, s:s + n], rstd[:, 0:n], ALU.mult)
                p.tt(t[:, 0:n], t[:, 0:n], nmr[:, 0:n], ALU.add)
                dst_fn(k, s, n, g, t[:, 0:n])

    p.dma_group([("sp", x[:, k, :], xT[k * 128:(k + 1) * 128, :]) for k in range(16)])
    modulate(0, 1)

    for k in range(8):
        for (s, n, g) in tts:
            t = tmp_r.next()
            p.dma("sp", t[:, 0:n], hy[k * 128:(k + 1) * 128, s:s + n])
            p.copy(brv(k, s, n), t[:, 0:n], eng="pool")
    for bi, (fD, bD, gD, gfun) in enumerate([(hgf, hgb, hgg, AF.Sigmoid), (dnf, dnb, dnz, AF.Silu)]):
        for k in range(8):
            rs = slice(k * 128, (k + 1) * 128)
            for (s, n, g) in tts:
                tf, tb, tg, sq = tmp_r.next(), tmp_r.next(), tmp_r.next(), tmp_r.next()
                p.dma("sp", tf[:, 0:n], fD[rs, s:s + n])
                p.dma("sp", tb[:, 0:n], bD[rs, s:s + n])
                p.dma("sp", tg[:, 0:n], gD[rs, s:s + n])
                p.tt(tf[:, 0:n], tf[:, 0:n], tb[:, 0:n], ALU.add)
                p.act(tg[:, 0:n], tg[:, 0:n], gfun)
                ps = pr.next()
                p.act(sq[:, 0:n], tf[:, 0:n], AF.Square)
                p.mm(ps[:, 0:n], ones[:, :], sq[:, 0:n])
                p.act(sq[:, 0:n], ps[:, 0:n], AF.Copy, scale=1.0 / 128)
                p.ts(sq[:, 0:n], sq[:, 0:n], RMS_EPS, ALU.add)
                p.act(sq[:, 0:n], sq[:, 0:n], AF.Sqrt)
                p.recip(sq[:, 0:n], sq[:, 0:n])
                p.tt(sq[:, 0:n], sq[:, 0:n], tf[:, 0:n], ALU.mult)
                p.stt(brv(8 + bi * 8 + k, s, n), sq[:, 0:n], nw[:, bi:bi + 1], tg[:, 0:n], ALU.mult, ALU.mult)

    if dbg:
        d_u = p.dram("d_u", [D, T], BF16, "ExternalOutput")
        d_br = p.dram("d_br", [3072, T], BF16, "ExternalOutput")
        d_m = p.dram("d_m", [D, T], BF16, "ExternalOutput")
        d_r = p.dram("d_r", [D, T], F32, "ExternalOutput")
        d_x1 = p.dram("d_x1", [D, T], F32, "ExternalOutput")
        for k in range(16):
            p.dma("sp", d_u[k * 128:(k + 1) * 128, :], u[:, k, :])
        for k in range(24):
            p.dma("sp", d_br[k * 128:(k + 1) * 128, :], brv(k, 0, T))
    macc = p.sbuf([128, T], F32, "macc")
    for dg in range(8):
        wgt, wbt = [], []
        for g in range(3):
            a = wr16.next()
            load_w_cast(p, a, wg[:, g * D + dg * 256: g * D + (dg + 1) * 256], 16)
            b = wr8.next()
            load_w_cast(p, b, wb[g * 1024:(g + 1) * 1024, dg * 256:(dg + 1) * 256], 8)
            wgt.append(a)
            wbt.append(b)
        for c in range(2):
            dt = dg * 2 + c
            for g in range(3):
                sgs = {}

                def ev_gate(ti, ps, sgs=sgs):
                    s, n, _ = tts[ti]
                    sg = tmp_r.next()
                    p.act(sg[:, 0:n], ps[:, 0:n], AF.Sigmoid)
                    sgs[ti] = sg

                def ev_proj(ti, ps, sgs=sgs, g=g, dt=dt):
                    s, n, _ = tts[ti]
                    if g == 0:
                        p.tt(macc[:, s:s + n], ps[:, 0:n], sgs[ti][:, 0:n], ALU.mult)
                    else:
                        p.tt(sgs[ti][:, 0:n], ps[:, 0:n], sgs[ti][:, 0:n], ALU.mult)
                        if g == 1:
                            p.tt(macc[:, s:s + n], macc[:, s:s + n], sgs[ti][:, 0:n], ALU.add, eng="pool")
                        else:
                            p.tt(m[:, dt, s:s + n], macc[:, s:s + n], sgs[ti][:, 0:n], ALU.add, eng="pool")

                gemm(wgt[g], c * 128, 16, lambda k, s, n: u[:, k, s:s + n], ev_gate)
                gemm(wbt[g], c * 128, 8, lambda k, s, n, g=g: brv(g * 8 + k, s, n), ev_proj)

    if dbg:
        for k in range(16):
            p.dma("sp", d_m[k * 128:(k + 1) * 128, :], m[:, k, :])
    p.dma_group([("sp", x[:, k, :], xT[k * 128:(k + 1) * 128, :]) for k in range(16)])
    for k in range(16):
        p.act(x[:, k, :], x[:, k, :], AF.Copy, scale=ALPHA_C)
    for dg in range(8):
        a = wr16.next()
        load_w_cast(p, a, wo[:, dg * 256:(dg + 1) * 256], 16)
        for c in range(2):
            dt = dg * 2 + c

            def ev(ti, ps, dt=dt):
                s, n, g = tts[ti]
                p.stt(x[:, dt, s:s + n], ps[:, 0:n], mcol(g, 2, dt), x[:, dt, s:s + n], ALU.mult, ALU.add)

            gemm(a, c * 128, 16, lambda k, s, n: m[:, k, s:s + n], ev)

    if dbg:
        for k in range(16):
            p.dma("sp", d_r[k * 128:(k + 1) * 128, :], x[:, k, :])
    def ln1_dst(k, s, n, g, t):
        p.act(x[:, k, s:s + n], t, AF.Identity, scale=ln[:, k:k + 1], bias=ln[:, 16 + k:17 + k])

    stats_ln(x, LN_EPS, 16, ln1_dst)
    if dbg:
        for k in range(16):
            p.dma("sp", d_x1[k * 128:(k + 1) * 128, :], x[:, k, :])
        p.wait_all("sp", [d_u, d_br, d_m, d_r, d_x1])
        p.emit()
        return nc
    modulate(3, 4)
    for k in range(16):
        p.act(x[:, k, :], x[:, k, :], AF.Copy, scale=ALPHA_C)

    h = m
    for q in range(4):
        for hg_ in range(8):
            a = wr16.next()
            load_w_cast(p, a, wf1[:, q * D + hg_ * 256: q * D + (hg_ + 1) * 256], 16)
            for c in range(2):
                ht = hg_ * 2 + c

                def ev(ti, ps, ht=ht):
                    s, n, g = tts[ti]
                    sq = tmp_r.next()
                    p.act(sq[:, 0:n], ps[:, 0:n], AF.Square)
                    p.stt(h[:, ht, s:s + n], ps[:, 0:n], 0.0, sq[:, 0:n], ALU.is_gt, ALU.mult)

                gemm(a, c * 128, 16, lambda k, s, n: u[:, k, s:s + n], ev)
        for dg in range(8):
            a = wr16.next()
            load_w_cast(p, a, wf2[q * D:(q + 1) * D, dg * 256:(dg + 1) * 256], 16)
            for c in range(2):
                dt = dg * 2 + c

                def ev(ti, ps, dt=dt):
                    s, n, g = tts[ti]
                    p.stt(x[:, dt, s:s + n], ps[:, 0:n], mcol(g, 5, dt), x[:, dt, s:s + n], ALU.mult, ALU.add)

                gemm(a, c * 128, 16, lambda k, s, n: h[:, k, s:s + n], ev)

    orr = Ring([p.sbuf([128, 512], F32) for _ in range(3)])

    def ln2_dst(k, s, n, g, t):
        o = orr.next()
        p.act(o[:, 0:n], t, AF.Identity, scale=ln[:, 32 + k:33 + k], bias=ln[:, 48 + k:49 + k])
        p.dma("sp", outT[k * 128:(k + 1) * 128, s:s + n], o[:, 0:n])

    stats_ln(x, LN_EPS, 16, ln2_dst)
    p.wait_all("sp", [outT])
    p.emit()
    return nc


def build_hg(NCH, NS, lbflag):
    nc, p = new_prog()
    L = NCH * 64
    W = NCH * 128
    di = lambda n, s: p.dram(n, s, F32, "ExternalInput")
    fzT, qT = di("fzT", [NS, 128, L]), di("qT", [NS, 128, L])
    fztm, itm = di("fztm", [NS, 64, W]), di("itm", [NS, 64, W])
    lgc = di("lgc", [NS, 128, 2])
    lgr = di("lgr", [NS, 64, 1024])
    rmask = di("rmask", [128, L])
    sud = di("su", [64, 64])
    mkd = di("mk", [64, 64])
    oT = p.dram("oT", [NS, 128, L], F32, "ExternalOutput")
    stf = p.dram("stf", [NS, 128, 128], F32, "ExternalOutput")

    F = [p.sbuf([128, L], F32, f"F{i}") for i in range(5)]
    KL = p.sbuf([64, W], F32, "KL")
    IT = p.sbuf([64, W], F32, "IT")
    rm = p.sbuf([128, L], F32, "rm")
    su, mk = p.sbuf([64, 64], F32, "su_sb"), p.sbuf([64, 64], F32, "mk_sb")
    p.dma("sp", rm[:, :], rmask[:, :])
    p.dma("sp", su[:, :], sud[:, :])
    p.dma("sp", mk[:, :], mkd[:, :])
    lc = p.sbuf([128, 4], F32, "lc")
    lr = p.sbuf([64, 1024], F32, "lr")
    lbr = p.sbuf([64, 512], F32, "lbr")
    omr = p.sbuf([64, 512], F32, "omr")
    S = p.sbuf([128, 128], F32, "S")
    bm = p.sbuf([128, NCH], F32, "bm")
    smr = Ring([p.sbuf([128, 128], F32) for _ in range(2)])
    pr = Ring([p.psum([128, 512], name=f"bank{i}") for i in range(8)])
    tr = Ring([p.sbuf([64, 512], F32) for _ in range(4)])
    ar = Ring([p.sbuf([64, 64], F32) for _ in range(3)])
    for a_ in ar.bufs:
        p.memset(a_[:, :], 0.0)

    for s_ in range(NS):
        p.dma("sp", lc[:, 0:2], lgc[s_, :, :])
        p.dma("sp", lr[:, :], lgr[s_, :, :])
        p.tt(lc[:, 2:3], lc[:, 1:2], lc[:, 0:1], ALU.subtract)
        p.act(lc[:, 2:3], lc[:, 2:3], AF.Sigmoid)
        p.ts(lc[:, 2:3], lc[:, 2:3], float(lbflag), ALU.mult)
        p.ts(lc[:, 3:4], lc[:, 2:3], -1.0, ALU.mult, 1.0, ALU.add)
        p.tt(lbr[:, :], lr[:, 512:1024], lr[:, 0:512], ALU.subtract)
        p.act(lbr[:, :], lbr[:, :], AF.Sigmoid)
        p.ts(lbr[:, :], lbr[:, :], float(lbflag), ALU.mult)
        p.ts(omr[:, :], lbr[:, :], -1.0, ALU.mult, 1.0, ALU.add)
        lb, om = lc[:, 2:3], lc[:, 3:4]
        p.dma("sp", F[0][:, :], fzT[s_, :, :])
        p.dma("sp", F[3][:, :], qT[s_, :, :])
        p.dma("sp", IT[:, :], itm[s_, :, :])
        p.act(F[0][:, :], F[0][:, :], AF.Sigmoid)
        p.ts(F[1][:, :], F[0][:, :], om, ALU.mult, lb, ALU.add)
        p.act(F[1][:, :], F[1][:, :], AF.Ln)
        p.scan(F[2][:, :], rm[:, :], F[1][:, :], 0.0, ALU.mult, ALU.add)
        p.act(F[1][:, :], F[2][:, :], AF.Exp)
        p.ts(F[4][:, :], F[0][:, :], -1.0, ALU.mult, 1.0, ALU.add)
        p.ts(F[4][:, :], F[4][:, :], om, ALU.mult)
        p.copy(bm[:, :], F[2][:, :].rearrange("p (c w) -> p c w", w=64)[:, :, 31])
        for c in range(NCH):
            cs_ = slice(c * 64, (c + 1) * 64)
            p.ts(F[2][:, cs_], F[2][:, cs_], bm[:, c:c + 1], ALU.subtract)
        p.act(F[0][:, :], F[2][:, :], AF.Exp)
        p.act(F[3][:, :], F[3][:, :], AF.Silu)
        p.tt(F[3][:, :], F[3][:, :], F[0][:, :], ALU.mult)
        p.act(F[2][:, :], F[2][:, :], AF.Exp, scale=-1.0)
        p.tt(F[4][:, :], F[4][:, :], F[2][:, :], ALU.mult)
        QeT, KdT, ebc, OT = F[3], F[4], F[1], F[0]
        for g in range(NCH // 4):
            cs = slice(g * 512, (g + 1) * 512)
            t1, t2 = tr.next(), tr.next()
            p.dma("sp", t1[:, :], fztm[s_, :, cs])
            p.act(t1[:, :], t1[:, :], AF.Sigmoid)
            p.tt(t2[:, :], t1[:, :], omr[:, :], ALU.mult)
            p.tt(t2[:, :], t2[:, :], lbr[:, :], ALU.add)
            p.act(t2[:, :], t2[:, :], AF.Ln)
            ps = pr.next()
            p.mm(ps[0:64, :], su[:, :], t2[:, :])
            p.act(t2[:, :], ps[0:64, :], AF.Exp)
            p.ts(t1[:, :], t1[:, :], -1.0, ALU.mult, 1.0, ALU.add)
            p.tt(t1[:, :], t1[:, :], omr[:, :], ALU.mult)
            p.tt(KL[:, cs], t1[:, :], t2[:, :], ALU.mult)
        p.memset(S[:, :], 0.0)
        for c in range(NCH):
            ts_ = slice(c * 64, (c + 1) * 64)
            fs = slice(c * 128, (c + 1) * 128)
            psA, psO, psS = pr.next(), pr.next(), pr.next()
            t0_ = c * 64
            p.mm(psA[0:32, 0:32], KdT[:, t0_:t0_ + 32], QeT[:, t0_:t0_ + 32])
            p.mm(psA[0:64, 32:64], KdT[:, ts_], QeT[:, t0_ + 32:t0_ + 64])
            A = ar.next()
            p.tt(A[0:32, 0:32], psA[0:32, 0:32], mk[0:32, 0:32], ALU.mult)
            p.tt(A[:, 32:64], psA[0:64, 32:64], mk[:, 32:64], ALU.mult)
            Sm = smr.next()
            p.ts(Sm[:, :], S[:, :], ebc[:, c * 64 + 31:c * 64 + 32], ALU.mult)
            p.mm(psO[:, 0:64], IT[:, fs], A[:, :], start=True, stop=False)
            p.mm(psO[:, 0:64], Sm[:, :], QeT[:, ts_], start=False, stop=True)
            p.mm(psS[:, 0:128], KL[:, fs], IT[:, fs])
            p.act(OT[:, ts_], psO[:, 0:64], AF.Copy)
            p.stt(S[:, :], S[:, :], ebc[:, c * 64 + 63:c * 64 + 64], psS[:, 0:128], ALU.mult, ALU.add)
        p.dma("sp", oT[s_, :, :], OT[:, :])
        p.dma("sp", stf[s_, :, :], S[:, :])
    p.wait_all("sp", [oT, stf])
    p.emit()
    return nc


def build_dn(NCH, NS):
    nc, p = new_prog()
    L = NCH * 64
    W = NCH * 128
    di = lambda n, s: p.dram(n, s, F32, "ExternalInput")
    kTd, qTd = di("kT", [NS, 128, L]), di("qT", [NS, 128, L])
    ktmd, vtmd = di("ktm", [NS, 64, W]), di("vtm", [NS, 64, W])
    arowd, browd = di("arow", [NS, 1, L]), di("brow", [NS, 1, L])
    acold, bcold = di("acol", [NS, 64, NCH]), di("bcol", [NS, 64, NCH])
    prmd = di("prm", [NS, 128, 2])
    rmaskd = di("rmask", [1, L])
    cst = di("cst", [128, 128 * 2 + 64 * 5])
    oT = p.dram("oT", [NS, 128, L], F32, "ExternalOutput")
    stf = p.dram("stf", [NS, 128, 128], F32, "ExternalOutput")

    C = p.sbuf([128, 128 * 2 + 64 * 5], F32, "cst_sb")
    p.dma("sp", C[:, :], cst[:, :])
    ident, ones = C[:, 0:128], C[:, 128:256]
    TI, SU = C[0:64, 256:320], C[0:64, 320:384]
    NEGs, NEGi, NEGsT = C[0:64, 384:448], C[0:64, 448:512], C[0:64, 512:576]
    id64 = C[0:64, 0:64]
    onesrow = C[0:1, 128:192]
    rmr = p.sbuf([1, min(L, 1088)], F32, "rmr")
    p.dma("sp", rmr[:, :], rmaskd[:, 0:min(L, 1088)])

    KT, QT, QI = [p.sbuf([128, L], F32, n) for n in ("KT", "QT", "QI")]
    otr = Ring([p.sbuf([128, 64], F32) for _ in range(3)])
    KTM, VTM = p.sbuf([64, W], F32, "KTM"), p.sbuf([64, W], F32, "VTM")
    RA = p.sbuf([65, L], F32, "RA")
    rows = {"gc": RA[0:1, :], "ngc": RA[32:33, :], "ab": RA[64:65, :]}
    rpb = {"gc": 0, "ngc": 32, "ab": 64}
    rtmp = Ring([p.sbuf([1, 1088], F32) for _ in range(2)])
    cols = {n: p.sbuf([64, NCH], F32, "col_" + n) for n in ("g", "beta", "gc", "ngc", "ab", "suf", "rk", "sk", "sl", "t")}
    prm = p.sbuf([128, 3], F32, "prm_sb")
    sdec = p.sbuf([128, NCH], F32, "sdec")
    S = p.sbuf([128, 128], F32, "S")
    pr = Ring([p.psum([128, 512], name=f"bank{i}") for i in range(8)])
    fr = Ring([p.sbuf([128, 512], F32) for _ in range(3)])
    m64 = Ring([p.sbuf([64, 64], F32) for _ in range(12)])
    rr = Ring([p.sbuf([64, 256], F32) for _ in range(3)])
    vr = Ring([p.sbuf([64, 128], F32) for _ in range(4)])
    wtr = Ring([p.sbuf([128, 64], F32) for _ in range(2)])

    def softplus_g(dst, src, P):
        p.act(dst, src, AF.Exp, bias=prm[0:P, 1:2])
        p.act(dst, dst, AF.Ln, bias=1.0)
        p.ts(dst, dst, prm[0:P, 2:3], ALU.mult)

    for s_ in range(NS):
        p.dma("sp", prm[:, 0:2], prmd[s_, :, :])
        p.act(prm[:, 2:3], prm[:, 0:1], AF.Exp)
        p.ts(prm[:, 2:3], prm[:, 2:3], -1.0, ALU.mult)
        R = rows
        SEG = L if L <= 1088 else 1088
        for t0 in range(0, L, SEG):
            sg = slice(t0, t0 + SEG)
            ta, tb = rtmp.next(), rtmp.next()
            p.dma("sp", ta[:, 0:SEG], arowd[s_, :, sg])
            p.dma("sp", tb[:, 0:SEG], browd[s_, :, sg])
            softplus_g(ta[:, 0:SEG], ta[:, 0:SEG], 1)
            p.scan(RA[0:1, sg], rmr[:, 0:SEG], ta[:, 0:SEG], 0.0, ALU.mult, ALU.add)
            p.ts(ta[:, 0:SEG], RA[0:1, sg], -1.0, ALU.mult)
            p.dma("sp", RA[32:33, sg], ta[:, 0:SEG])
            p.act(tb[:, 0:SEG], tb[:, 0:SEG], AF.Sigmoid)
            p.act(tb[:, 0:SEG], tb[:, 0:SEG], AF.Ln)
            p.tt(tb[:, 0:SEG], tb[:, 0:SEG], RA[0:1, sg], ALU.add)
            p.dma("sp", RA[64:65, sg], tb[:, 0:SEG])
        Cc = cols
        p.dma("sp", Cc["g"][:, :], acold[s_, :, :])
        p.dma("sp", Cc["beta"][:, :], bcold[s_, :, :])
        softplus_g(Cc["g"][:, :], Cc["g"][:, :], 64)
        p.act(Cc["beta"][:, :], Cc["beta"][:, :], AF.Sigmoid)
        ps = pr.next()
        p.mm(ps[0:64, 0:NCH], TI, Cc["g"][:, :])
        p.copy(Cc["gc"][:, :], ps[0:64, 0:NCH])
        p.ts(Cc["ngc"][:, :], Cc["gc"][:, :], -1.0, ALU.mult)
        p.act(Cc["t"][:, :], Cc["beta"][:, :], AF.Ln)
        p.tt(Cc["ab"][:, :], Cc["gc"][:, :], Cc["t"][:, :], ALU.add)
        ps = pr.next()
        p.mm(ps[0:64, 0:NCH], SU, Cc["g"][:, :])
        p.act(Cc["suf"][:, :], ps[0:64, 0:NCH], AF.Exp)
        p.dma("sp", KT[:, :], kTd[s_, :, :])
        p.dma("sp", QT[:, :], qTd[s_, :, :])
        p.dma("sp", KTM[:, :], ktmd[s_, :, :])
        p.dma("sp", VTM[:, :], vtmd[s_, :, :])
        for c in range(NCH):
            sq = vr.next()
            p.act(sq[:, :], KTM[:, c * 128:(c + 1) * 128], AF.Square, accum_out=Cc["rk"][:, c:c + 1])
        p.ts(Cc["rk"][:, :], Cc["rk"][:, :], 1e-6, ALU.add)
        p.act(Cc["rk"][:, :], Cc["rk"][:, :], AF.Sqrt)
        p.recip(Cc["rk"][:, :], Cc["rk"][:, :])
        p.act(Cc["t"][:, :], Cc["gc"][:, :], AF.Exp)
        p.tt(Cc["sk"][:, :], Cc["t"][:, :], Cc["beta"][:, :], ALU.mult)
        p.tt(Cc["sk"][:, :], Cc["sk"][:, :], Cc["rk"][:, :], ALU.mult)
        p.tt(Cc["sl"][:, :], Cc["suf"][:, :], Cc["rk"][:, :], ALU.mult)
        for t0 in range(0, L, 512):
            n = min(512, L - t0)
            sl = slice(t0, t0 + n)
            for (X, scl) in ((KT, 1.0), (QT, 128.0 ** -0.5)):
                sq, ps = fr.next(), pr.next()
                p.act(sq[:, 0:n], X[:, sl], AF.Square)
                p.mm(ps[:, 0:n], ones, sq[:, 0:n])
                p.ts(sq[:, 0:n], ps[:, 0:n], 1e-6, ALU.add)
                p.act(sq[:, 0:n], sq[:, 0:n], AF.Sqrt)
                p.recip(sq[:, 0:n], sq[:, 0:n])
                p.stt(X[:, sl], X[:, sl], scl, sq[:, 0:n], ALU.mult, ALU.mult)
            ps = pr.next()
            p.mm(ps[:, 0:n], C[0:1, 128:256], R["gc"][:, sl])
            eg = fr.next()
            p.act(eg[:, 0:n], ps[:, 0:n], AF.Exp)
            p.tt(QI[:, sl], QT[:, sl], eg[:, 0:n], ALU.mult)
            for c in range(t0 // 64, (t0 + n) // 64):
                p.copy(sdec[:, c:c + 1], eg[:, (c * 64 + 63 - t0):(c * 64 + 64 - t0)])
        p.memset(S[:, :], 0.0)
        for c in range(NCH):
            ts_ = slice(c * 64, (c + 1) * 64)
            fs = slice(c * 128, (c + 1) * 128)
            cc = slice(c, c + 1)

            def emat(rown, neg, biasv):
                ps = pr.next()
                pb = rpb[rown]
                p.mm(ps[0:64, 0:64], C[pb:pb + 1, 128:192], R[rown][:, ts_], start=True, stop=False)
                p.mm(ps[0:64, 0:64], id64, neg, start=False, stop=True)
                e = m64.next()
                p.act(e[:, :], ps[0:64, 0:64], AF.Exp, bias=biasv)
                return e

            E1 = emat("ab", NEGs, Cc["ngc"][:, cc])
            E1T = emat("ngc", NEGsT, Cc["ab"][:, cc])
            E2 = emat("gc", NEGi, Cc["ngc"][:, cc])
            psK, psQ = pr.next(), pr.next()
            p.mm(psK[0:64, 0:64], KT[:, ts_], KT[:, ts_])
            p.mm(psQ[0:64, 0:64], KT[:, ts_], QT[:, ts_])
            MT, M, AT = m64.next(), m64.next(), m64.next()
            p.stt(MT[:, :], psK[0:64, 0:64], -1.0, E1[:, :], ALU.mult, ALU.mult)
            p.stt(M[:, :], psK[0:64, 0:64], -1.0, E1T[:, :], ALU.mult, ALU.mult)
            p.tt(AT[:, :], psQ[0:64, 0:64], E2[:, :], ALU.mult)
            r = rr.next()
            p.ts(r[:, 0:128], VTM[:, fs], Cc["beta"][:, cc], ALU.mult)
            p.ts(r[:, 128:256], KTM[:, fs], Cc["sk"][:, cc], ALU.mult)
            KL = vr.next()
            p.ts(KL[:, :], KTM[:, fs], Cc["sl"][:, cc], ALU.mult)
            for st in range(6):
                ps = pr.next()
                p.mm(ps[0:64, 0:256], MT[:, :], r[:, :])
                r2 = rr.next()
                p.tt(r2[:, :], ps[0:64, 0:256], r[:, :], ALU.add)
                r = r2
                if st < 5:
                    ps2, ps3 = pr.next(), pr.next()
                    p.mm(ps2[0:64, 0:64], MT[:, :], M[:, :])
                    p.mm(ps3[0:64, 0:64], M[:, :], MT[:, :])
                    M2, MT2 = m64.next(), m64.next()
                    p.act(M2[:, :], ps2[0:64, 0:64], AF.Copy)
                    p.copy(MT2[:, :], ps3[0:64, 0:64])
                    M, MT = M2, MT2
            psT = pr.next()
            p.transpose(psT[:, 0:64], r[:, 128:256], id64)
            WT = wtr.next()
            p.act(WT[:, :], psT[:, 0:64], AF.Copy)
            psV, psO, psS = pr.next(), pr.next(), pr.next()
            p.mm(psV[0:64, 0:128], WT[:, :], S[:, :])
            Vn = vr.next()
            p.tt(Vn[:, :], r[:, 0:128], psV[0:64, 0:128], ALU.subtract)
            p.mm(psO[:, 0:64], Vn[:, :], AT[:, :], start=True, stop=False)
            p.mm(psO[:, 0:64], S[:, :], QI[:, ts_], start=False, stop=True)
            p.mm(psS[:, 0:128], KL[:, :], Vn[:, :])
            ot = otr.next()
            p.act(ot[:, :], psO[:, 0:64], AF.Copy)
            p.dma("sp", oT[s_, :, ts_], ot[:, :])
            p.stt(S[:, :], S[:, :], sdec[:, cc], psS[:, 0:128], ALU.mult, ALU.add)
        p.dma("sp", stf[s_, :, :], S[:, :])
    p.wait_all("sp", [oT, stf])
    p.emit()
    return nc


def dn_consts():
    j = np.arange(64)
    c = np.zeros((128, 128 * 2 + 64 * 5), np.float32)
    c[:, 0:128] = np.eye(128)
    c[:, 128:256] = 1.0
    c[0:64, 256:320] = (j[:, None] <= j[None, :])
    c[0:64, 320:384] = (j[:, None] > j[None, :])
    c[0:64, 384:448] = -30000.0 * (j[:, None] >= j[None, :])
    c[0:64, 448:512] = -30000.0 * (j[:, None] > j[None, :])
    c[0:64, 512:576] = -30000.0 * (j[None, :] >= j[:, None])
    return c


import math
import ml_dtypes
PI = math.pi


def build_hy(Ls):
    nc, p = new_prog()
    di = lambda n, s, dt=F32: p.dram(n, s, dt, "ExternalInput")
    w1a, w2d, b2d, w3d = di("w1a", [34, 64]), di("w2", [64, 64]), di("b2", [64, 1]), di("w3", [64, 512])
    skd = di("skipb", [2, 128, 256])
    cn = di("cn", [1, 256])
    W1, W2, B2, W3 = p.sbuf([34, 64], F32, "W1"), p.sbuf([64, 64], F32, "W2"), p.sbuf([64, 1], F32, "B2"), p.sbuf([64, 512], F32, "W3")
    SK = p.sbuf([128, 2, 256], F32, "SK")
    CN = p.sbuf([1, 256], F32, "CN")
    for a, b in ((W1, w1a), (W2, w2d), (B2, b2d), (W3, w3d), (CN, cn)):
        p.dma("sp", a[:, :], b[:, :])
    for n in range(2):
        p.dma("sp", SK[:, n, :], skd[n, :, :])
    pr = Ring([p.psum([128, 512], name=f"bank{i}") for i in range(8)])
    tr = Ring([p.sbuf([128, 512], F32) for _ in range(8)])
    mats = Ring([p.sbuf([128, 32, 256], BF16) for _ in range(3)])
    WB = p.sbuf([128, 32, 256], BF16, "WB")
    YR, YI = p.sbuf([128, 32, 256], BF16, "YR"), p.sbuf([128, 32, 256], BF16, "YI")
    KR, KI = p.sbuf([128, 32, 128], F32, "KR"), p.sbuf([128, 32, 128], F32, "KI")
    HS, HD = p.sbuf([128, 32, 2, 128], BF16, "HS"), p.sbuf([128, 32, 2, 128], BF16, "HD")
    HID1 = YR[:, :, :].rearrange("p a b -> p (a b)").bitcast(F32)[0:64, :]
    HID2 = YI[:, :, :].rearrange("p a b -> p (a b)").bitcast(F32)[0:64, :]
    HB0 = p.sbuf([1, 256], F32, "HB0")
    HB0b = p.sbuf([1, 256], BF16, "HB0b")
    NYQ = p.sbuf([1, 512], F32, "NYQ")
    YN = p.sbuf([1, 256], BF16, "YN")
    CNb = p.sbuf([1, 256], BF16, "CNb")
    p.copy(CNb[:, :], CN[:, :])
    outs = []

    def sin_rr(dst, src, bias=None):
        P, n = dst.ap.shape[0], dst.ap.shape[1]
        x, m1 = tr.next(), tr.next()
        if bias is None:
            p.copy(x[0:P, 0:n], src)
        else:
            p.ts(x[0:P, 0:n], src, bias, ALU.add)
        p.ts(m1[0:P, 0:n], x[0:P, 0:n], PI, ALU.is_gt, -2 * PI, ALU.mult)
        p.tt(m1[0:P, 0:n], m1[0:P, 0:n], x[0:P, 0:n], ALU.add)
        p.ts(x[0:P, 0:n], x[0:P, 0:n], -PI, ALU.is_lt, 2 * PI, ALU.mult)
        p.tt(x[0:P, 0:n], x[0:P, 0:n], m1[0:P, 0:n], ALU.add)
        p.act(dst, x[0:P, 0:n], AF.Sin)

    for L in Ls:
        nT = L // 128
        sfx = str(L)
        featsT = di("featsT" + sfx, [34, L])
        win = di("win" + sfx, [L, 512])
        vD, g1D, g2D = di("v" + sfx, [L, 256]), di("g1" + sfx, [L, 256]), di("g2" + sfx, [L, 256])
        Cf, Sf = di("Cf" + sfx, [L, L], BF16), di("Sf" + sfx, [L, L], BF16)
        Ci, Si = di("Ci" + sfx, [L, L], BF16), di("Si" + sfx, [L, L], BF16)
        alt = di("alt" + sfx, [L, 2], BF16)
        altN = di("altN" + sfx, [1, L], BF16)
        y1D = p.dram("y1s" + sfx, [L, 256], F32, "Internal")
        oD = p.dram("o" + sfx, [L, 256], F32, "ExternalOutput")
        outs.append(oD)
        ALT = p.sbuf([128, 32, 2], BF16, "ALT" + sfx)
        ALTN = p.sbuf([1, L], BF16, "ALTN" + sfx)
        p.dma_group([("sp", ALT[:, k, :], alt[k * 128:(k + 1) * 128, :]) for k in range(nT)])
        p.dma("sp", ALTN[:, :], altN[:, :])
        FT = tr.next()
        for t0 in range(0, L, 512):
            n = min(512, L - t0)
            ft = tr.next()
            p.dma("sp", ft[0:34, 0:n], featsT[:, t0:t0 + n])
            ps = pr.next()
            p.mm(ps[0:64, 0:n], W1[:, :], ft[0:34, 0:n])
            sin_rr(HID1[:, t0:t0 + n], ps[0:64, 0:n])
            ps = pr.next()
            p.mm(ps[0:64, 0:n], W2[:, :], HID1[:, t0:t0 + n])
            sin_rr(HID2[:, t0:t0 + n], ps[0:64, 0:n], bias=B2[:, 0:1])
        for k in range(nT):
            ps = pr.next()
            p.mm(ps[:, :], HID2[:, k * 128:(k + 1) * 128], W3[:, :])
            wt, hw = tr.next(), tr.next()
            p.dma("sp", wt[:, :], win[k * 128:(k + 1) * 128, :])
            p.tt(hw[:, :], ps[:, :], wt[:, :], ALU.mult)
            for n_ in range(2):
                f_, b_ = hw[:, n_ * 256:n_ * 256 + 128], hw[:, n_ * 256 + 128:n_ * 256 + 256]
                p.tt(HS[:, k, n_, :], f_, b_, ALU.add)
                p.tt(HD[:, k, n_, :], f_, b_, ALU.subtract, eng="pool")
                if k == 0:
                    p.copy(HB0[0:1, n_ * 128:(n_ + 1) * 128], hw[0:1, n_ * 256 + 128:n_ * 256 + 256])
        p.copy(HB0b[:, :], HB0[:, :])

        def load_mat(M_, c0):
            mt = mats.next()
            p.dma_group([("sp", mt[:, k, :], M_[k * 128:(k + 1) * 128, c0:c0 + 256]) for k in range(nT)])
            return mt

        for k in range(nT):
            t = tr.next()
            p.dma("sp", t[:, 0:256], vD[k * 128:(k + 1) * 128, :])
            p.copy(WB[:, k, :], t[:, 0:256])
        for n_ in range(2):
            gD = g1D if n_ == 0 else g2D
            srcD = vD if n_ == 0 else y1D
            dstD = y1D if n_ == 0 else oD
            for fg in range(0, L, 256):
                mc, ms = load_mat(Cf, fg), load_mat(Sf, fg)
                for j in range(2):
                    if fg + j * 128 >= L:
                        continue
                    ft_ = fg // 128 + j
                    psr, psi = pr.next(), pr.next()
                    for k in range(nT):
                        p.mm(psr[:, 0:128], mc[:, k, j * 128:(j + 1) * 128], HS[:, k, n_, :], start=(k == 0), stop=False)
                    p.mm(psr[:, 0:128], CNb[0:1, 0:128], HB0b[0:1, n_ * 128:(n_ + 1) * 128], start=False, stop=True)
                    for k in range(nT):
                        p.mm(psi[:, 0:128], ms[:, k, j * 128:(j + 1) * 128], HD[:, k, n_, :], start=(k == 0), stop=(k == nT - 1))
                    p.act(KR[:, ft_, :], psr[:, 0:128], AF.Copy)
                    p.copy(KI[:, ft_, :], psi[:, 0:128])
            psn = pr.next()
            for k in range(nT):
                p.mm(psn[0:1, 0:128], ALT[:, k, 0:1], HS[:, k, n_, :], start=(k == 0), stop=(k == nT - 1))
            p.tt(NYQ[0:1, 256 + n_ * 128:256 + (n_ + 1) * 128], psn[0:1, 0:128], HB0[0:1, n_ * 128:(n_ + 1) * 128], ALU.subtract)
            for fg in range(0, L, 256):
                mc, ms = load_mat(Cf, fg), load_mat(Sf, fg)
                for j in range(2):
                    if fg + j * 128 >= L:
                        continue
                    ft_ = fg // 128 + j
                    psr, psi = pr.next(), pr.next()
                    for k in range(nT):
                        p.mm(psr[:, 0:256], mc[:, k, j * 128:(j + 1) * 128], WB[:, k, :], start=(k == 0), stop=(k == nT - 1))
                    for k in range(nT):
                        p.mm(psi[:, 0:256], ms[:, k, j * 128:(j + 1) * 128], WB[:, k, :], start=(k == 0), stop=(k == nT - 1))
                    for b in range(2):
                        bs = slice(b * 128, (b + 1) * 128)
                        a1, a2, a3, a4 = tr.next(), tr.next(), tr.next(), tr.next()
                        p.tt(a1[:, 0:128], psr[:, bs], KR[:, ft_, :], ALU.mult)
                        p.tt(a2[:, 0:128], psi[:, bs], KI[:, ft_, :], ALU.mult)
                        p.tt(YR[:, ft_, bs], a1[:, 0:128], a2[:, 0:128], ALU.subtract, eng="pool")
                        p.tt(a3[:, 0:128], psr[:, bs], KI[:, ft_, :], ALU.mult)
                        p.tt(a4[:, 0:128], psi[:, bs], KR[:, ft_, :], ALU.mult)
                        p.tt(YI[:, ft_, bs], a3[:, 0:128], a4[:, 0:128], ALU.add, eng="pool")
            psn = pr.next()
            for k in range(nT):
                p.mm(psn[0:1, 0:256], ALT[:, k, 0:1], WB[:, k, :], start=(k == 0), stop=(k == nT - 1))
            for b in range(2):
                p.tt(YN[0:1, b * 128:(b + 1) * 128], psn[0:1, b * 128:(b + 1) * 128], NYQ[0:1, 256 + n_ * 128:256 + (n_ + 1) * 128], ALU.mult)
            for tg in range(0, L, 256):
                mc, ms = load_mat(Ci, tg), load_mat(Si, tg)
                for j in range(2):
                    if tg + j * 128 >= L:
                        continue
                    tt_ = tg // 128 + j
                    ps = pr.next()
                    for k in range(nT):
                        p.mm(ps[:, 0:256], mc[:, k, j * 128:(j + 1) * 128], YR[:, k, :], start=(k == 0), stop=False)
                    for k in range(nT):
                        p.mm(ps[:, 0:256], ms[:, k, j * 128:(j + 1) * 128], YI[:, k, :], start=False, stop=False)
                    p.mm(ps[:, 0:256], ALTN[0:1, tt_ * 128:(tt_ + 1) * 128], YN[0:1, :], start=False, stop=True)
                    wv, gv = tr.next(), tr.next()
                    p.dma("sp", wv[:, 0:256], srcD[tt_ * 128:(tt_ + 1) * 128, :])
                    p.dma("sp", gv[:, 0:256], gD[tt_ * 128:(tt_ + 1) * 128, :])
                    p.tt(wv[:, 0:256], wv[:, 0:256], SK[:, n_, :], ALU.mult)
                    p.tt(wv[:, 0:256], wv[:, 0:256], ps[:, 0:256], ALU.add)
                    p.tt(wv[:, 0:256], wv[:, 0:256], gv[:, 0:256], ALU.mult)
                    p.dma("sp", dstD[tt_ * 128:(tt_ + 1) * 128, :], wv[:, 0:256])
                    if n_ == 0:
                        p.copy(WB[:, tt_, :], wv[:, 0:256], eng="pool")
    p.wait_all("sp", outs)
    p.emit()
    return nc


def hy_consts(L):
    N = 2 * L
    t = np.arange(L, dtype=np.float64)
    th = 2 * np.pi * np.outer(t, t) / N
    bf = ml_dtypes.bfloat16
    C = np.cos(th)
    Sn = -np.sin(th)
    wf = np.full(L, 2.0 / N)
    wf[0] = 1.0 / N
    Ci = (wf[:, None] * np.cos(th))
    Si = (-wf[:, None] * np.sin(th))
    alt = np.zeros((L, 2))
    alt[:, 0] = (-1.0) ** t
    altN = ((-1.0) ** t / N)[None, :]
    d = {"Cf": C.astype(bf), "Sf": Sn.astype(bf), "Ci": Ci.astype(bf), "Si": Si.astype(bf), "alt": alt.astype(bf), "altN": altN.astype(bf)}
    tt = np.linspace(0.0, 1.0, L, dtype=np.float32)[:, None]
    ang = (2.0 * np.pi / L) * np.arange(L, dtype=np.float32)[:, None]
    bands = np.linspace(1e-4, 15, 16, dtype=np.float32)[None, :]
    feats = np.concatenate([tt, np.cos(bands * ang), -np.sin(bands * ang), np.ones_like(tt)], axis=-1).astype(np.float32)
    d["featsT"] = np.ascontiguousarray(feats.T)
    deltas = np.abs(np.linspace(math.log(1e-2) / 1.5, math.log(1e-2) / 0.3, 1024, dtype=np.float32))
    d["window_full"] = np.exp(-tt * deltas).astype(np.float32)
    return d


NTOK = 8704
NCHS = 68
_CACHE = {}


def _prog(key, fn):
    if key not in _CACHE:
        _CACHE[key] = fn()
    return _CACHE[key]


def _mods(c, c_ctx, w_ada, b_ada):
    nc = _prog("mod", build_mod)
    v = np.stack([c[0], c[1], c_ctx], 1)
    vT = np.ascontiguousarray(v.reshape(16, 128, 3).transpose(1, 0, 2).reshape(128, 48))
    maps = []
    for i in range(8):
        sl = slice(i * 1536, (i + 1) * 1536)
        maps.append({"vT": vT, "w": np.ascontiguousarray(w_ada[:, :, sl]),
                     "b3": np.ascontiguousarray(np.repeat(b_ada[:, None, sl], 3, 1))})
    res = run(nc, maps)
    return np.concatenate([r["mod"] for r in res], axis=2)


def _inproj(xT, I, l, mod_l):
    nc = _prog("inproj", lambda: build_inproj(13 * 128))
    grp = [mod_l[2], mod_l[0], mod_l[1]]
    sh = np.concatenate([pk(g[0:2048]) for g in grp], 1)
    sc = np.concatenate([pk(g[2048:4096]) for g in grp], 1)
    maps = []
    for h in range(8):
        cw = np.zeros((128, 39), np.float32)
        cb = np.zeros((128, 13), np.float32)
        qk = slice((h // 2) * 128, (h // 2) * 128 + 128)
        hs = slice(h * 128, h * 128 + 128)
        cw[:, 9:12] = I["dn_conv_k"][l][:, qk].T
        cw[:, 12:15] = I["dn_conv_v"][l][:, hs].T
        cw[:, 21:24] = I["dn_conv_q"][l][:, qk].T
        for j in range(3):
            cs = slice(j * 1024 + h * 128, j * 1024 + h * 128 + 128)
            cw[:, (9 + j) * 3:(9 + j) * 3 + 3] = I["hy_conv_w"][l][:, cs].T
            cb[:, 9 + j] = I["hy_conv_b"][l][cs]
        maps.append({"xT": xT, "w": head_weight(I["w_in"][l], h), "sc": sc, "sh": sh, "cw": cw, "cb": cb})
    res = run(nc, maps)
    return np.stack([r["PT"] for r in res]).reshape(8, 13, 128, NTOK)


def _stream_idx():
    out = []
    for b in range(2):
        cidx = b * 256 + np.arange(256)
        lidx = 512 + b * 4096 + np.arange(4096)
        out.append(np.concatenate([cidx, lidx]))
        out.append(np.concatenate([cidx[::-1], lidx[::-1]]))
    return out


def _tm(a):
    L = a.shape[1]
    return np.ascontiguousarray(a.T.reshape(L // 64, 64, 128).transpose(1, 0, 2).reshape(64, (L // 64) * 128))


def _unstream(oT):
    idx = _stream_idx()
    f = np.zeros((128, NTOK), np.float32)
    b = np.zeros((128, NTOK), np.float32)
    for s_ in range(4):
        (f if s_ % 2 == 0 else b)[:, idx[s_]] = oT[s_]
    return f, b


def _hg(PT, I, l):
    nc = _prog(("hg", l), lambda: build_hg(NCHS, 4, 1.0 if l == 1 else 0.0))
    idx = _stream_idx()
    L = NCHS * 64
    rmask = np.ones((128, L), np.float32)
    rmask[:, ::64] = 0
    j = np.arange(64)
    su = (j[:, None] > j[None, :]).astype(np.float32)
    mk = (j[:, None] <= j[None, :]).astype(np.float32)
    lg = I["hg_lb_logits"]
    maps = []
    for h in range(8):
        hs = slice(h * 128, h * 128 + 128)
        fz = np.stack([PT[h, s_ % 2][:, idx[s_]] for s_ in range(4)])
        q = np.stack([PT[h, 5][:, idx[s_]] for s_ in range(4)])
        iv = np.stack([PT[h, 2][:, idx[s_]] for s_ in range(4)])
        lgc = np.stack([np.stack([lg[s_ % 2, 0, hs], lg[s_ % 2, 1, hs]], 1) for s_ in range(4)]).astype(np.float32)
        lgr = np.stack([np.concatenate([np.tile(lg[s_ % 2, 0, hs], (64, 4)), np.tile(lg[s_ % 2, 1, hs], (64, 4))], 1)
                        for s_ in range(4)]).astype(np.float32)
        maps.append({"fzT": np.ascontiguousarray(fz), "qT": np.ascontiguousarray(q),
                     "fztm": np.stack([_tm(fz[s_]) for s_ in range(4)]), "itm": np.stack([_tm(iv[s_]) for s_ in range(4)]),
                     "lgc": lgc, "lgr": lgr, "rmask": rmask, "su": su, "mk": mk})
    res = run(nc, maps)
    F = np.zeros((1024, NTOK), np.float32)
    B = np.zeros((1024, NTOK), np.float32)
    for h in range(8):
        F[h * 128:(h + 1) * 128], B[h * 128:(h + 1) * 128] = _unstream(res[h]["oT"])
    return F, B


def _dn(PT, I, l):
    nc = _prog("dn", lambda: build_dn(NCHS, 4))
    idx = _stream_idx()
    L = NCHS * 64
    rmask = np.ones((1, L), np.float32)
    rmask[:, ::64] = 0
    cst = dn_consts()
    col = lambda x: np.ascontiguousarray(x.reshape(NCHS, 64).T)
    maps = []
    for h in range(8):
        k = np.stack([PT[h, 3][:, idx[s_]] for s_ in range(4)])
        q = np.stack([PT[h, 7][:, idx[s_]] for s_ in range(4)])
        v = np.stack([PT[h, 4][:, idx[s_]] for s_ in range(4)])
        a = np.stack([PT[h, 12][0 + s_ % 2][idx[s_]] for s_ in range(4)])
        b = np.stack([PT[h, 12][2 + s_ % 2][idx[s_]] for s_ in range(4)])
        prm = np.stack([np.stack([np.full(128, I["dn_a_log"][l, s_ % 2, h]), np.full(128, I["dn_dt_bias"][l, s_ % 2, h])], 1)
                        for s_ in range(4)]).astype(np.float32)
        maps.append({"kT": np.ascontiguousarray(k), "qT": np.ascontiguousarray(q),
                     "ktm": np.stack([_tm(k[s_]) for s_ in range(4)]), "vtm": np.stack([_tm(v[s_]) for s_ in range(4)]),
                     "arow": np.ascontiguousarray(a[:, None, :]), "brow": np.ascontiguousarray(b[:, None, :]),
                     "acol": np.stack([col(a[s_]) for s_ in range(4)]), "bcol": np.stack([col(b[s_]) for s_ in range(4)]),
                     "prm": prm, "rmask": rmask, "cst": cst})
    res = run(nc, maps)
    F = np.zeros((1024, NTOK), np.float32)
    B = np.zeros((1024, NTOK), np.float32)
    for h in range(8):
        F[h * 128:(h + 1) * 128], B[h * 128:(h + 1) * 128] = _unstream(res[h]["oT"])
    return F, B


def _hy(PT, I, l):
    Ls = [256, 4096]
    nc = _prog("hy", lambda: build_hy(Ls))
    if "hyK" not in _CACHE:
        _CACHE["hyK"] = {L: hy_consts(L) for L in Ls}
    K = _CACHE["hyK"]
    maps = []
    for h in range(8):
        hs = slice(h * 128, h * 128 + 128)
        w1a = np.concatenate([I["hy_filt_w1"][l], I["hy_filt_b1"][l][None, :]], 0).astype(np.float32)
        w3 = np.ascontiguousarray(I["hy_filt_w3"][l].reshape(64, 2, 2, 1024)[:, :, :, hs].reshape(64, 512))
        skipb = np.stack([np.tile(I["hy_skip"][l][n, hs], (128, 2)) for n in range(2)]).astype(np.float32)
        cn = np.concatenate([-np.ones((1, 128)), np.ones((1, 128))], 1).astype(np.float32)
        m = {"w1a": w1a, "w2": np.ascontiguousarray(I["hy_filt_w2"][l]), "b2": np.ascontiguousarray(I["hy_filt_b2"][l][:, None]),
             "w3": w3, "skipb": skipb, "cn": cn}
        for L in Ls:
            s = str(L)
            base = 0 if L == 256 else 512
            tmj = lambda tile: np.ascontiguousarray(
                np.concatenate([PT[h, tile][:, base + b * L: base + (b + 1) * L].T for b in range(2)], 1))
            kk = K[L]
            m.update({"featsT" + s: kk["featsT"], "win" + s: np.ascontiguousarray(np.tile(kk["window_full"][:, hs], (1, 4))),
                      "v" + s: tmj(9), "g1" + s: tmj(10), "g2" + s: tmj(11),
                      "Cf" + s: kk["Cf"], "Sf" + s: kk["Sf"], "Ci" + s: kk["Ci"], "Si" + s: kk["Si"],
                      "alt" + s: kk["alt"], "altN" + s: kk["altN"]})
        maps.append(m)
    res = run(nc, maps)
    HY = np.zeros((1024, NTOK), np.float32)
    for h in range(8):
        for L in Ls:
            base = 0 if L == 256 else 512
            o = res[h]["o" + str(L)]
            for b in range(2):
                HY[h * 128:(h + 1) * 128, base + b * L: base + (b + 1) * L] = o[:, b * 128:(b + 1) * 128].T
    return HY


def _post(xT, PT, HGF, HGB, DNF, DNB, HY, I, l, mod_l):
    nc = _prog("post", lambda: build_post(True))
    G = np.concatenate([PT[h, 6] for h in range(8)], 0)
    Z = np.concatenate([PT[h, 8] for h in range(8)], 0)
    wg = np.ascontiguousarray(I["w_in"][l][:, OFF["gate"]:])
    wb = np.ascontiguousarray(I["w_branch"][l].reshape(3072, 2048))
    lnP = np.concatenate([pk(I[n][l]) for n in ("ln1_g", "ln1_b", "ln2_g", "ln2_b")], 1)
    nwv = np.ascontiguousarray(np.stack([I["hg_norm_w"][l], I["dn_norm_w"][l]], 1))
    maps = []
    colsets = []
    for i in range(8):
        cols = np.concatenate([i * 64 + np.arange(64), 512 + i * 1024 + np.arange(1024)])
        colsets.append(cols)
        mg = [mod_l[2], mod_l[0 if i < 4 else 1]]
        mc = []
        for g in range(2):
            v = mg[g]
            for j in range(6):
                ch = v[j * 2048:(j + 1) * 2048]
                mc.append(pk(ch))
        sel = lambda A: np.ascontiguousarray(A[:, cols])
        maps.append({"xT": sel(xT), "mods": np.concatenate(mc, 1), "hgf": sel(HGF), "hgb": sel(HGB), "hgg": sel(G),
                     "dnf": sel(DNF), "dnb": sel(DNB), "dnz": sel(Z), "hy": sel(HY), "nw": nwv,
                     "wg": wg, "wb": wb, "wo": np.ascontiguousarray(I["w_out"][l]), "ln": lnP,
                     "wf1": np.ascontiguousarray(I["w_ff1"][l]), "wf2": np.ascontiguousarray(I["w_ff2"][l])})
    res = run(nc, maps)
    xT_new = np.zeros_like(xT)
    for i in range(8):
        xT_new[:, colsets[i]] = res[i]["oT"]
    return xT_new


def kernel(**inputs):
    I = {k: np.asarray(v) for k, v in inputs.items()}
    mod = _mods(I["c"], I["c_ctx"], I["w_ada"], I["b_ada"])
    x, ctx = I["x"], I["ctx"]
    xT = np.ascontiguousarray(np.concatenate([ctx[0], ctx[1], x[0], x[1]], 0).T)
    dbg = os.environ.get("K_DEBUG_DIR")
    for l in range(2):
        PT = _inproj(xT, I, l, mod[l])
        HGF, HGB = _hg(PT, I, l)
        DNF, DNB = _dn(PT, I, l)
        HY = _hy(PT, I, l)
        xT = _post(xT, PT, HGF, HGB, DNF, DNB, HY, I, l, mod[l])
        if dbg:
            np.save(f"{dbg}/xT_{l}.npy", xT)
            np.save(f"{dbg}/HY_{l}.npy", HY)
            np.save(f"{dbg}/HG_{l}.npy", np.stack([HGF, HGB]))
            np.save(f"{dbg}/DN_{l}.npy", np.stack([DNF, DNB]))
    out = xT[:, 512:].T.reshape(2, 4096, 2048)
    return np.ascontiguousarray(out.astype(np.float32))
```
